# Optimizing a Trainium2 kernel written in Bass

```python
import jax, jax.numpy as jnp
from jax import lax
import numpy as np

D_MODEL = 1024
BATCH = 8
SEQ = 2048
DEPTH = 4

HEAD_DIM = 64
CONV_WIDTH = 3 * D_MODEL // 8
POOL_WIDTH = D_MODEL // 4
SGU_WIDTH = D_MODEL - CONV_WIDTH - POOL_WIDTH
CONV_HEADS = CONV_WIDTH // HEAD_DIM
SGU_HEADS = SGU_WIDTH // HEAD_DIM
POOL_WINDOWS = (2, 4, 8, 16)
POOL_GROUPS = len(POOL_WINDOWS)
POOL_GROUP_DIM = POOL_WIDTH // POOL_GROUPS
CONV_K = 3
CHUNK = 128
MIX_WIDTH = CONV_WIDTH + POOL_WIDTH + SGU_WIDTH
IN_WIDTH = 3 * CONV_WIDTH + POOL_WIDTH + 2 * SGU_WIDTH
D_FF = -(-8 * D_MODEL // (3 * 256)) * 256
ALPHA = float((2 * DEPTH) ** 0.25)
BETA = float((8 * DEPTH) ** -0.25)
LN_EPS = 1e-5

kernel_name = "hybrid_conv_pool_sgu_deepnorm"


def _norm_stats(x, eps=LN_EPS):
    xf = x.astype(jnp.float32)
    mu = jnp.mean(xf, axis=-1, keepdims=True)
    var = jnp.mean(jnp.square(xf - mu), axis=-1, keepdims=True)
    return ((xf - mu) * lax.rsqrt(var + eps)).astype(x.dtype)


def layer_norm(x, g, b):
    return _norm_stats(x) * g + b


def short_gated_conv(xa, gb, gc, w_conv):
    z = gc * xa
    s = z.shape[1]
    zp = jnp.pad(z, ((0, 0), (CONV_K - 1, 0), (0, 0)))
    y = w_conv[0] * zp[:, 0:s] + w_conv[1] * zp[:, 1:s + 1] + w_conv[2] * zp[:, 2:s + 2]
    return gb * y


def multiscale_pool(p, w_pool, pool_scale):
    b, s, _ = p.shape
    pg = p.reshape(b, s, POOL_GROUPS, POOL_GROUP_DIM)
    cs = jnp.cumsum(pg.astype(jnp.float32), axis=1)
    t1 = jnp.arange(1, s + 1, dtype=jnp.float32)
    means = []
    for g, w in enumerate(POOL_WINDOWS):
        csg = cs[:, :, g]
        csp = jnp.pad(csg, ((0, 0), (w, 0), (0, 0)))
        win_sum = csp[:, w:] - csp[:, :s]
        count = jnp.minimum(t1, float(w))[None, :, None]
        means.append(win_sum / count)
    mean = jnp.stack(means, axis=2).astype(p.dtype)
    d = mean - pg
    y = jnp.einsum('bsgc,gcd->bsgd', d, w_pool)
    return y.reshape(b, s, POOL_WIDTH) * pool_scale


def chunked_sgu(uv, sgu_ln_g, w_spatial, b_spatial):
    b, s, _ = uv.shape
    uv = jax.nn.gelu(uv, approximate=False)
    u, v = uv[..., :SGU_WIDTH], uv[..., SGU_WIDTH:]
    vh = v.reshape(b, s, SGU_HEADS, HEAD_DIM)
    vh = _norm_stats(vh) * sgu_ln_g.reshape(SGU_HEADS, HEAD_DIM)
    vc = vh.reshape(b, s // CHUNK, CHUNK, SGU_HEADS, HEAD_DIM)
    mask = jnp.tril(jnp.ones((CHUNK, CHUNK), dtype=w_spatial.dtype))
    wm = w_spatial * mask
    mixed = jnp.einsum('hts,bnshd->bnthd', wm, vc) + b_spatial.T[None, None, :, :, None]
    return u * mixed.reshape(b, s, SGU_WIDTH)


def swiglu(h, w_gate_up, w_down):
    gu = h @ w_gate_up
    g, u = gu[..., :D_FF], gu[..., D_FF:]
    return (jax.nn.silu(g) * u) @ w_down


def setup_inputs(seed: int = 0) -> dict:
    key = jax.random.key(seed)
    ks = jax.random.split(key, 16)
    f32 = jnp.float32
    nrm = lambda k, shape: jax.random.normal(k, shape, dtype=f32)
    x = nrm(ks[0], (BATCH, SEQ, D_MODEL))
    w_in = nrm(ks[1], (DEPTH, D_MODEL, IN_WIDTH)) * D_MODEL ** -0.5
    w_conv = nrm(ks[2], (DEPTH, CONV_K, CONV_WIDTH)) * CONV_K ** -0.5
    w_pool = nrm(ks[3], (DEPTH, POOL_GROUPS, POOL_GROUP_DIM, POOL_GROUP_DIM)) * POOL_GROUP_DIM ** -0.5
    pool_scale = 1.0 + 0.1 * nrm(ks[4], (DEPTH, POOL_WIDTH))
    sgu_ln_g = 1.0 + 0.1 * nrm(ks[5], (DEPTH, SGU_WIDTH))
    w_spatial = nrm(ks[6], (DEPTH, SGU_HEADS, CHUNK, CHUNK)) * CHUNK ** -0.5
    b_spatial = 1.0 + 0.1 * nrm(ks[7], (DEPTH, SGU_HEADS, CHUNK))
    w_o = nrm(ks[8], (DEPTH, MIX_WIDTH, D_MODEL)) * (MIX_WIDTH ** -0.5) * BETA
    ln1_g = 1.0 + 0.1 * nrm(ks[9], (DEPTH, D_MODEL))
    ln1_b = 0.02 * nrm(ks[10], (DEPTH, D_MODEL))
    w_gate_up = nrm(ks[11], (DEPTH, D_MODEL, 2 * D_FF)) * D_MODEL ** -0.5
    w_down = nrm(ks[12], (DEPTH, D_FF, D_MODEL)) * (D_FF ** -0.5) * BETA
    ln2_g = 1.0 + 0.1 * nrm(ks[13], (DEPTH, D_MODEL))
    ln2_b = 0.02 * nrm(ks[14], (DEPTH, D_MODEL))
    return {"x": x, "w_in": w_in, "w_conv": w_conv, "w_pool": w_pool,
            "pool_scale": pool_scale, "sgu_ln_g": sgu_ln_g, "w_spatial": w_spatial,
            "b_spatial": b_spatial, "w_o": w_o, "ln1_g": ln1_g, "ln1_b": ln1_b,
            "w_gate_up": w_gate_up, "w_down": w_down, "ln2_g": ln2_g, "ln2_b": ln2_b}


def reference(x, w_in, w_conv, w_pool, pool_scale, sgu_ln_g, w_spatial, b_spatial,
              w_o, ln1_g, ln1_b, w_gate_up, w_down, ln2_g, ln2_b):
    c0 = CONV_WIDTH
    for l in range(DEPTH):
        proj = x @ w_in[l]
        xa = proj[..., 0:c0]
        gb = proj[..., c0:2 * c0]
        gc = proj[..., 2 * c0:3 * c0]
        p = proj[..., 3 * c0:3 * c0 + POOL_WIDTH]
        uv = proj[..., 3 * c0 + POOL_WIDTH:]
        ya = short_gated_conv(xa, gb, gc, w_conv[l])
        yb = multiscale_pool(p, w_pool[l], pool_scale[l])
        yc = chunked_sgu(uv, sgu_ln_g[l], w_spatial[l], b_spatial[l])
        mix = jnp.concatenate([ya, yb, yc], axis=-1) @ w_o[l]
        h = layer_norm(ALPHA * x + mix, ln1_g[l], ln1_b[l])
        x = layer_norm(ALPHA * h + swiglu(h, w_gate_up[l], w_down[l]), ln2_g[l], ln2_b[l])
    return x
```

```python
import numpy as np
import concourse.bass as bass
import concourse.mybir as mybir
from concourse.bass_utils import run_bass_kernel_spmd

F32 = mybir.dt.float32
BF16 = mybir.dt.bfloat16
AF = mybir.ActivationFunctionType
ALU = mybir.AluOpType

COMPUTE = ("pe", "act", "dve", "pool")


class Op:
    __slots__ = ("eng", "fn", "dma", "deps", "milestone", "count", "sem", "idx", "name")

    def __init__(self, eng, fn, dma, name):
        self.eng = eng
        self.fn = fn
        self.dma = dma
        self.deps = []
        self.milestone = False
        self.count = 0
        self.sem = None
        self.idx = 0
        self.name = name


class Sched:
    def __init__(self, n_dma_sems=12):
        self.streams = {e: [] for e in ("pe", "act", "dve", "pool", "sp")}
        self.last_writer = {}
        self.readers = {}
        self.n_dma_sems = n_dma_sems
        self.dma_rr = {}
        self.dma_last = {}
        self.nops = 0
        self._cap = None

    def capture(self, fn):
        assert self._cap is None
        self._cap = []
        fn()
        ops, self._cap = self._cap, None
        return ops

    def replay_merged(self, a, b):
        ia = ib = 0
        while ia < len(a) or ib < len(b):
            if ib >= len(b) or (ia < len(a) and ia * len(b) <= ib * len(a)):
                self.add(*a[ia])
                ia += 1
            else:
                self.add(*b[ib])
                ib += 1

    def add(self, eng, fn, reads=(), writes=(), dma=False, name=""):
        if self._cap is not None:
            self._cap.append((eng, fn, tuple(reads), tuple(writes), dma, name))
            return None
        op = Op(eng, fn, dma, name)
        op.idx = self.nops
        self.nops += 1
        deps = {}

        def consider(d, raw):
            if d is None or d is op:
                return
            if not d.dma and not op.dma and d.eng == op.eng:
                if op.eng == "pe":
                    return
                if not raw:
                    return
            deps[id(d)] = d

        for r in reads:
            consider(self.last_writer.get(r), True)
        for r in writes:
            consider(self.last_writer.get(r), False)
            for rd in self.readers.get(r, ()):
                consider(rd, False)
        if dma:
            q = self.dma_rr.get(eng, 0)
            self.dma_rr[eng] = q + 1
            slot = (eng, q % self.n_dma_sems)
            prev = self.dma_last.get(slot)
            op.sem = slot
            op.count = (prev.count if prev is not None else 0) + 16
            if prev is not None:
                deps[id(prev)] = prev
            self.dma_last[slot] = op
        for d in deps.values():
            if not d.dma:
                d.milestone = True
        op.deps = list(deps.values())
        for r in writes:
            self.last_writer[r] = op
            self.readers[r] = []
        for r in reads:
            self.readers.setdefault(r, []).append(op)
        self.streams[eng].append(op)
        return op

    def emit(self, nc, sems, dma_sems, final_waits):
        for e in COMPUTE:
            c = 0
            for op in self.streams[e]:
                if not op.dma and op.milestone:
                    c += 1
                    op.count = c
        engs = {"pe": "tensor", "act": "scalar", "dve": "vector", "pool": "gpsimd", "sp": "sync"}

        def run(ename, eng):
            waited = {}
            for op in self.streams[ename]:
                for d in op.deps:
                    s = dma_sems[d.sem] if d.dma else sems[d.eng]
                    key = d.sem if d.dma else d.eng
                    if waited.get(key, 0) < d.count:
                        eng.wait_ge(s, d.count)
                        waited[key] = d.count
                inst = op.fn(eng)
                if op.dma:
                    inst.then_inc(dma_sems[op.sem], 16)
                elif op.milestone:
                    inst.then_inc(sems[ename], 1)
            if ename == "sp":
                for d in final_waits:
                    eng.wait_ge(dma_sems[d.sem], d.count)

        with nc.Block() as block:
            for ename, attr in engs.items():
                if not self.streams[ename] and ename != "sp":
                    continue
                getattr(block, attr)(lambda eng, ename=ename: run(ename, eng))


DEPTH = 4
D = 1024
SEQ = 2048
NT = 16
NG = 4
C0 = 384
PW = 256
SW = 384
INW = 2176
DFF = 2816
NFF = DFF // 128
ALPHA = float((2 * DEPTH) ** 0.25)
EPS = 1e-5
POOL_WINDOWS = (2, 4, 8, 16)
RSLOTS = 6
SLABS = ((0, 6), (6, 8), (14, 8))
AX = mybir.AxisListType.X


def _const_inputs():
    ident = np.eye(128, dtype=np.float32)
    maskT = np.triu(np.ones((128, 128), dtype=np.float32))
    pinv = np.zeros((128, 2, 16), dtype=np.float32)
    pinvw = np.zeros((128, 2), dtype=np.float32)
    for c in range(2):
        for p in range(128):
            w = POOL_WINDOWS[2 * c + p // 64]
            pinvw[p, c] = 1.0 / w
            for t in range(16):
                pinv[p, c, t] = 1.0 / min(t + 1, w)
    flag = np.zeros((128, 1), dtype=np.float32)
    flag[64:] = 1.0
    return {"c_ident": ident, "c_maskT": maskT, "c_pinv": pinv, "c_pinvw": pinvw, "c_flag": flag}


def build_program(layers, dump=None):
    from contextlib import ExitStack

    nc = bass.Bass("TRN2", target_bir_lowering=False)
    dr = lambda name, shape, dt=F32, kind="ExternalInput": nc.dram_tensor(name, shape, dt, kind=kind)
    x_d = dr("x", [SEQ, D]).ap()
    w_in_d = dr("w_in", [DEPTH, D, INW]).ap()
    w_convT_d = dr("w_convT", [DEPTH, 128, 3, 3]).ap()
    w_pool_d = dr("w_pool", [DEPTH, 4, 64, 64]).ap()
    pscale_d = dr("pool_scaleT", [DEPTH, 128, 2]).ap()
    sgu_g_h = dr("sgu_ln_g", [DEPTH, SW])
    w_spT_d = dr("w_spatialT", [DEPTH, 128, 6, 128]).ap()
    b_spT_d = dr("b_spatialT", [DEPTH, 128, 6]).ap()
    w_o_d = dr("w_o", [DEPTH, D, D]).ap()
    ln_h = {k: dr(k, [DEPTH, D]) for k in ("ln1_g", "ln1_b", "ln2_g", "ln2_b")}
    w_gu_d = dr("w_gate_up", [DEPTH, D, 2 * DFF]).ap()
    w_dn_d = dr("w_down", [DEPTH, DFF, D]).ap()
    c_ident_d = dr("c_ident", [128, 128]).ap()
    c_maskT_d = dr("c_maskT", [128, 128]).ap()
    c_pinv_d = dr("c_pinv", [128, 2, 16]).ap()
    c_pinvw_d = dr("c_pinvw", [128, 2]).ap()
    c_flag_d = dr("c_flag", [128, 1]).ap()
    out_d = dr("out", [SEQ, D], F32, "ExternalOutput").ap()
    dump_d = None
    if dump is not None:
        dump_d = dr("dump", dump[1], dump[2], "ExternalOutput").ap()

    S = Sched()
    with ExitStack() as es:
        sb = lambda n, s, d=F32: es.enter_context(nc.sbuf_tensor(n, s, d))
        ps = lambda n, s, d=F32: es.enter_context(nc.psum_tensor(n, s, d))
        X = sb("X", [128, NT, D])
        XB = [sb(f"XB{p}", [128, D], BF16) for p in range(2)]
        XT = sb("XT", [128, 8, SEQ], BF16)
        BIG = sb("BIG", [128, 8, SEQ], BF16)
        RING = [sb(f"W{r}", [128, 2048], BF16) for r in range(RSLOTS)]
        GBUF = sb("GBUF", [128, D])
        BBUF = sb("BBUF", [128, D])
        ident_f = sb("ident_f", [128, 128])
        ident_b = sb("ident_b", [128, 128], BF16)
        maskT = sb("maskT", [128, 128])
        WMT = sb("WMT", [128, 6, 128], BF16)
        BSP = [sb(f"BSP{p}", [128, 6]) for p in range(2)]
        GSG = sb("GSG", [128, SW])
        WC = [sb(f"WC{p}", [128, 3, 3]) for p in range(2)]
        PSC = [sb(f"PSC{p}", [128, 2]) for p in range(2)]
        WPBD = [sb(f"WPBD{p}", [128, 2, 128], BF16) for p in range(2)]
        PINV = sb("PINV", [128, 2, 16])
        PINVW = sb("PINVW", [128, 2])
        FLAG = sb("FLAG", [128, 1])
        CEPS = sb("CEPS", [128, 8])
        CNH = sb("CNH", [128, 8])
        U = [sb(f"U{p}", [128, SW]) for p in range(3)]
        V = [sb(f"V{p}", [128, SW]) for p in range(3)]
        SQ = [sb(f"SQ{p}", [128, SW]) for p in range(3)]
        VN = [sb(f"VN{p}", [128, SW], BF16) for p in range(2)]
        YC = [sb(f"YC{p}", [128, SW], BF16) for p in range(2)]
        SMALL = [sb(f"SMALL{p}", [128, 8, 6]) for p in range(3)]
        ZP = sb("ZP", [128, 16 + SEQ])
        NTT = 4
        T = [sb(f"T{k}", [128, 528]) for k in range(NTT)]
        DB = [sb(f"DB{p}", [128, 512], BF16) for p in range(2)]
        ST = [sb(f"ST{p}", [128, 2, 6]) for p in range(4)]
        MV = [sb(f"MV{p}", [128, 8]) for p in range(4)]
        CNEG = sb("CNEG", [128, 1])
        P = [ps(f"P{b}", [128, 512]) for b in range(6)]
        PT = [ps(f"PT{b}", [128, 8, 128], BF16) for b in range(2)]

        sems = {e: es.enter_context(nc.semaphore("s_" + e)) for e in COMPUTE}
        dsems = {}
        for q in ("sp", "pool"):
            for k in range(S.n_dma_sems):
                dsems[(q, k)] = es.enter_context(nc.semaphore(f"d_{q}{k}"))

        cnt = {"pt": 0, "t": 0}

        def next_T():
            k = cnt["t"] % NTT
            cnt["t"] += 1
            return k

        tiles = []

        def cols_src(w2d, c, n=128):
            return w2d[:, c:c + n].rearrange("(kc p) n -> p kc n", p=128)

        def plan_layer(l):
            idx = {}
            win = w_in_d[l]
            idx["uv"] = []
            for q in range(3):
                idx["uv"].append(len(tiles))
                tiles.append([(lambda s: s[:].rearrange("p (kc n) -> p kc n", n=256), cols_src(win, 3 * C0 + PW + q * 256, 256))])
            order = [3 * C0, 3 * C0 + 128]
            for j in range(3):
                order += [j * 128, 2 * C0 + j * 128, C0 + j * 128]
            idx["cv"] = []
            for q in range(0, len(order), 2):
                pair = order[q:q + 2]
                idx["cv"].append(len(tiles))
                tiles.append([((lambda s, h=h: s[:].rearrange("p (h kc n) -> p h kc n", h=2, n=128)[:, h]), cols_src(win, c))
                              for h, c in enumerate(pair)])
            idx["wo"] = []
            for kh in range(2):
                for nh in range(2):
                    idx["wo"].append(len(tiles))
                    src = w_o_d[l][kh * 512:(kh + 1) * 512, nh * 512:(nh + 1) * 512].rearrange("(kc p) n -> p kc n", p=128)
                    tiles.append([(lambda s: s[:].rearrange("p (kc n) -> p kc n", n=512), src)])
            idx["gu"] = {}
            idx["dn"] = {}
            for si, (m0, nm) in enumerate(SLABS):
                for m in range(m0, m0 + nm):
                    idx["gu"][m] = len(tiles)
                    tiles.append([((lambda s, h=h: s[:].rearrange("p (h kc n) -> p h kc n", h=2, n=128)[:, h]), cols_src(w_gu_d[l], c))
                                  for h, c in enumerate((m * 128, DFF + m * 128))])
                idx["dn"][si] = []
                for kk in range(0, nm, 2):
                    idx["dn"][si].append(len(tiles))
                    src = w_dn_d[l][(m0 + kk) * 128:(m0 + kk + 2) * 128, :].rearrange("(kc p) n -> p kc n", p=128)
                    tiles.append([(lambda s: s[:].rearrange("p (kc n) -> p kc n", n=1024), src)])
            return idx

        plans = {l: plan_layer(l) for l in layers}
        ring_state = {"next": 0, "free": list(range(RSLOTS)), "slot_of": {}}

        def ring_pump():
            while ring_state["free"] and ring_state["next"] < len(tiles):
                t = ring_state["next"]
                sl = ring_state["free"].pop(0)
                ring_state["slot_of"][t] = sl
                slot = RING[sl]
                for dstf, src in tiles[t]:
                    S.add("pool", lambda e, dstf=dstf, src=src, slot=slot: e.dma_start(out=dstf(slot), in_=src),
                          writes=[("W", sl)], dma=True, name=f"wload{t}")
                ring_state["next"] += 1

        def ring_release(ts):
            for t in ts:
                ring_state["free"].append(ring_state["slot_of"].pop(t))
            ring_pump()

        def wslot(t):
            sl = ring_state["slot_of"][t]
            return RING[sl], ("W", sl)

        def bcast_row(handle, l, n):
            return bass.AP(handle, l * n, [[0, 128], [1, n]])

        S.add("sp", lambda e: e.dma_start(out=ident_f[:], in_=c_ident_d), writes=["ident_f"], dma=True)
        S.add("sp", lambda e: e.dma_start(out=maskT[:], in_=c_maskT_d), writes=["maskT"], dma=True)
        for i in range(NT):
            S.add("sp", lambda e, i=i: e.dma_start(out=X[:, i, :], in_=x_d[i * 128:(i + 1) * 128, :]), writes=[("X", i)], dma=True)
        S.add("sp", lambda e: e.dma_start(out=PINV[:], in_=c_pinv_d), writes=["PINV"], dma=True)
        S.add("sp", lambda e: e.dma_start(out=PINVW[:], in_=c_pinvw_d), writes=["PINVW"], dma=True)
        S.add("sp", lambda e: e.dma_start(out=FLAG[:], in_=c_flag_d), writes=["FLAG"], dma=True)
        S.add("dve", lambda e: e.tensor_copy(out=ident_b[:], in_=ident_f[:]), reads=["ident_f"], writes=["ident_b"])
        S.add("pool", lambda e: e.memset(CEPS[:], EPS), writes=["CEPS"])
        S.add("pool", lambda e: e.memset(CNH[:], -0.5), writes=["CNH"])
        S.add("pool", lambda e: e.memset(CNEG[:], -1.0), writes=["CNEG"])
        S.add("pool", lambda e: e.memset(ZP[:, 0:16], 0.0), writes=["ZPhalo"])
        for p in range(2):
            S.add("pool", lambda e, p=p: e.memset(WPBD[p][:], 0.0), writes=[("WPBD", p)])
        ring_pump()

        def load_small(l):
            p = l % 2
            for k in range(2):
                uv_ = U[k][:].rearrange("p (h t) -> p h t", t=128)
                S.add("sp", lambda e, k=k, uv_=uv_: e.dma_start(out=uv_, in_=w_spT_d[l][:, 3 * k:3 * k + 3, :]), writes=[("U", k)], dma=True)
                S.add("dve", lambda e, k=k, uv_=uv_: e.tensor_tensor(out=WMT[:, 3 * k:3 * k + 3, :], in0=uv_, in1=maskT[:].unsqueeze(1).to_broadcast([128, 3, 128]), op=ALU.mult),
                      reads=[("U", k), "maskT"], writes=["WMT"])
            S.add("sp", lambda e: e.dma_start(out=BSP[p][:], in_=b_spT_d[l]), writes=[("BSP", p)], dma=True)
            S.add("sp", lambda e: e.dma_start(out=GSG[:], in_=bcast_row(sgu_g_h, l, SW)), writes=["GSG"], dma=True)
            S.add("sp", lambda e: e.dma_start(out=WC[p][:], in_=w_convT_d[l]), writes=[("WC", p)], dma=True)
            S.add("sp", lambda e: e.dma_start(out=PSC[p][:], in_=pscale_d[l]), writes=[("PSC", p)], dma=True)
            for c in range(2):
                for h in range(2):
                    S.add("pool", lambda e, c=c, h=h: e.dma_start(out=WPBD[p][h * 64:(h + 1) * 64, c, h * 64:(h + 1) * 64], in_=w_pool_d[l, 2 * c + h]),
                          writes=[("WPBD", p)], dma=True)

        def load_gb(l, which):
            S.add("sp", lambda e: e.dma_start(out=GBUF[:], in_=bcast_row(ln_h[f"ln{which}_g"], l, D)), writes=["GBUF"], dma=True)
            S.add("sp", lambda e: e.dma_start(out=BBUF[:], in_=bcast_row(ln_h[f"ln{which}_b"], l, D)), writes=["BBUF"], dma=True)

        def cast_stage(i):
            q = i % 2
            S.add("act", lambda e: e.activation(out=XB[q][:], in_=X[:, i, :], func=AF.Copy), reads=[("X", i)], writes=[("XB", q)])

        def transpose_stage(i):
            q = i % 2
            b = cnt["pt"] % 2
            cnt["pt"] += 1
            for k in range(8):
                S.add("pe", lambda e, k=k: e.transpose(PT[b][:, k, :], XB[q][:, k * 128:(k + 1) * 128], ident_b[:]),
                      reads=[("XB", q), "ident_b"], writes=[("PT", b)])
            S.add("act", lambda e: e.activation(out=XT[:, :, i * 128:(i + 1) * 128], in_=PT[b][:], func=AF.Copy),
                  reads=[("PT", b)], writes=[("XT", i)])

        def cast_transpose(i):
            cast_stage(i)
            transpose_stage(i)

        def ln_stats(i):
            p = i % 4
            for c in range(2):
                S.add("dve", lambda e, c=c: e.bn_stats(out=ST[p][:, c, :], in_=X[:, i, c * 512:(c + 1) * 512]),
                      reads=[("X", i)], writes=[("ST", p)])
            S.add("dve", lambda e: e.bn_aggr(out=MV[p][:, 0:2], in_=ST[p][:].rearrange("p a b -> p (a b)")),
                  reads=[("ST", p)], writes=[("MV", p)])
            S.add("pool", lambda e: e.tensor_tensor(out=MV[p][:, 2:3], in0=MV[p][:, 1:2], in1=CEPS[:, 0:1], op=ALU.add),
                  reads=[("MV", p), "CEPS"], writes=[("MVe", p)])
            S.add("pool", lambda e: e.tensor_tensor(out=MV[p][:, 3:4], in0=MV[p][:, 2:3], in1=CNH[:, 0:1], op=ALU.pow),
                  reads=[("MVe", p), "CNH"], writes=[("MVr", p)])
            S.add("pool", lambda e: e.tensor_tensor(out=MV[p][:, 4:5], in0=MV[p][:, 0:1], in1=MV[p][:, 3:4], op=ALU.mult),
                  reads=[("MV", p), ("MVr", p)], writes=[("MVm", p)])
            S.add("pool", lambda e: e.tensor_tensor(out=MV[p][:, 5:6], in0=MV[p][:, 4:5], in1=CNEG[:, 0:1], op=ALU.mult),
                  reads=[("MVm", p), "CNEG"], writes=[("MVn", p)])

        def ln_norm(i):
            p = i % 4
            S.add("act", lambda e: e.activation(out=X[:, i, :], in_=X[:, i, :], func=AF.Identity, scale=MV[p][:, 3:4], bias=MV[p][:, 5:6]),
                  reads=[("X", i), ("MVr", p), ("MVn", p)], writes=[("X", i)])
            S.add("pool", lambda e: e.tensor_tensor(out=X[:, i, :], in0=X[:, i, :], in1=GBUF[:], op=ALU.mult),
                  reads=[("X", i), "GBUF"], writes=[("X", i)])

        def ln_bias(i):
            S.add("dve", lambda e: e.tensor_tensor(out=X[:, i, :], in0=X[:, i, :], in1=BBUF[:], op=ALU.add),
                  reads=[("X", i), "BBUF"], writes=[("X", i)])

        v3 = lambda t: t[:].rearrange("p (h d) -> p h d", d=64)

        def sgu_a(l, i):
            pl = plans[l]
            p = i % 3
            A, Bk = P[0], P[1]
            (s0, r0), (s1, r1), (s2, r2) = [wslot(t) for t in pl["uv"]]
            v0 = s0[:].rearrange("p (kc n) -> p kc n", n=256)
            v1 = s1[:].rearrange("p (kc n) -> p kc n", n=256)
            v2 = s2[:].rearrange("p (kc n) -> p kc n", n=256)
            groups = [(A[:, 0:256], v0, 0, 256, r0, 0), (A[:, 256:384], v1, 0, 128, r1, 0),
                      (Bk[:, 0:128], v1, 128, 256, r1, 1), (Bk[:, 128:384], v2, 0, 256, r2, 1)]
            for (o, wv, ca, cb, rk, bank) in groups:
                for k in range(8):
                    S.add("pe", lambda e, o=o, wv=wv, ca=ca, cb=cb, k=k: e.matmul(o, lhsT=XT[:, k, i * 128:(i + 1) * 128], rhs=wv[:, k, ca:cb],
                                                                                 start=(k == 0), stop=(k == 7)),
                          reads=[("XT", i), rk], writes=[("P", bank)])
            S.add("act", lambda e: e.activation(out=U[p][:], in_=A[:, 0:SW], func=AF.Gelu), reads=[("P", 0)], writes=[("U", p)])
            S.add("act", lambda e: e.activation(out=V[p][:], in_=Bk[:, 0:SW], func=AF.Gelu), reads=[("P", 1)], writes=[("V", p)])
            S.add("act", lambda e: e.activation(out=SQ[p][:], in_=V[p][:], func=AF.Square), reads=[("V", p)], writes=[("SQ", p)])

        def sgu_b(l, i):
            p = i % 3
            sm = SMALL[p]
            S.add("dve", lambda e: e.tensor_reduce(out=sm[:, 0, :], in_=v3(V[p]), axis=AX, op=ALU.add), reads=[("V", p)], writes=[("sm0", p)])
            S.add("dve", lambda e: e.tensor_reduce(out=sm[:, 1, :], in_=v3(SQ[p]), axis=AX, op=ALU.add), reads=[("SQ", p)], writes=[("sm1", p)])
            S.add("dve", lambda e: e.tensor_scalar(out=sm[:, 2, :], in0=sm[:, 0, :], scalar1=1.0 / 64, scalar2=None, op0=ALU.mult),
                  reads=[("sm0", p)], writes=[("sm2", p)])
            S.add("dve", lambda e: e.tensor_tensor(out=sm[:, 3, :], in0=sm[:, 2, :], in1=sm[:, 2, :], op=ALU.mult),
                  reads=[("sm2", p)], writes=[("sm3", p)])
            S.add("dve", lambda e: e.scalar_tensor_tensor(out=sm[:, 4, :], in0=sm[:, 1, :], scalar=1.0 / 64, in1=sm[:, 3, :],
                                                          op0=ALU.mult, op1=ALU.subtract),
                  reads=[("sm1", p), ("sm3", p)], writes=[("sm4", p)])
            S.add("pool", lambda e: e.tensor_tensor(out=sm[:, 5, :], in0=sm[:, 4, :], in1=CEPS[:, 0:6], op=ALU.add),
                  reads=[("sm4", p), "CEPS"], writes=[("sm5", p)])
            S.add("pool", lambda e: e.tensor_tensor(out=sm[:, 6, :], in0=sm[:, 5, :], in1=CNH[:, 0:6], op=ALU.pow),
                  reads=[("sm5", p), "CNH"], writes=[("sm6", p)])

        def sgu_c1(l, i):
            p = i % 3
            q = i % 2
            sm = SMALL[p]
            S.add("dve", lambda e: e.tensor_tensor(out=v3(SQ[p]), in0=v3(V[p]), in1=sm[:, 2, :].unsqueeze(2).to_broadcast([128, 6, 64]), op=ALU.subtract),
                  reads=[("V", p), ("sm2", p), ("SQ", p)], writes=[("SQ", p)])
            S.add("dve", lambda e: e.tensor_tensor(out=v3(VN[q]), in0=v3(SQ[p]), in1=sm[:, 6, :].unsqueeze(2).to_broadcast([128, 6, 64]), op=ALU.mult),
                  reads=[("SQ", p), ("sm6", p)], writes=[("VN", q)])

        def sgu_mix(l, i):
            q = i % 2
            M = P[2]
            for h in range(6):
                S.add("pe", lambda e, h=h: e.matmul(M[:, h * 64:(h + 1) * 64], lhsT=WMT[:, h, :], rhs=VN[q][:, h * 64:(h + 1) * 64], start=True, stop=True),
                      reads=["WMT", ("VN", q)], writes=[("P", 2)])

        def sgu_c2(l, i):
            p = i % 3
            q = i % 2
            lp = l % 2
            M = P[2]
            S.add("dve", lambda e: e.tensor_tensor(out=SQ[p][:], in0=M[:, 0:SW], in1=GSG[:], op=ALU.mult),
                  reads=[("P", 2), "GSG", ("SQ", p)], writes=[("SQ", p)])
            S.add("dve", lambda e: e.tensor_tensor(out=v3(SQ[p]), in0=v3(SQ[p]), in1=BSP[lp][:].unsqueeze(2).to_broadcast([128, 6, 64]), op=ALU.add),
                  reads=[("SQ", p), ("BSP", lp)], writes=[("SQ", p)])
            S.add("dve", lambda e: e.tensor_tensor(out=YC[q][:], in0=SQ[p][:], in1=U[p][:], op=ALU.mult),
                  reads=[("SQ", p), ("U", p)], writes=[("YC", q)])

        def sgu_tr(l, i):
            q = i % 2
            b = cnt["pt"] % 2
            cnt["pt"] += 1
            for c in range(3):
                S.add("pe", lambda e, c=c: e.transpose(PT[b][:, c, :], YC[q][:, c * 128:(c + 1) * 128], ident_b[:]),
                      reads=[("YC", q), "ident_b"], writes=[("PT", b)])
            S.add("act", lambda e: e.activation(out=BIG[:, 5:8, i * 128:(i + 1) * 128], in_=PT[b][:, 0:3, :], func=AF.Copy),
                  reads=[("PT", b)], writes=[("BIG", 5, i), ("BIG", 6, i), ("BIG", 7, i)])

        def col_chunk_mm(l, q, n, bank):
            pl = plans[l]
            slot, rk = wslot(pl["cv"][q // 2])
            wv = slot[:].rearrange("p (h kc n) -> p h kc n", h=2, n=128)[:, q % 2]
            for k in range(8):
                S.add("pe", lambda e, k=k: e.matmul(P[bank][:], lhsT=wv[:, k, :], rhs=XT[:, k, n * 512:(n + 1) * 512], start=(k == 0), stop=(k == 7)),
                      reads=[("XT", 4 * n + t) for t in range(4)] + [rk], writes=[("P", bank)])

        def conv_unit(l, j, n):
            lp = l % 2
            if True:
                b0 = 3
                col_chunk_mm(l, 2 + 3 * j + 0, n, b0)
                col_chunk_mm(l, 2 + 3 * j + 1, n, b0 + 1)
                ta, tb, tc = next_T(), next_T(), next_T()
                zc = 16 + n * 512
                S.add("act", lambda e, ta=ta, b0=b0: e.activation(out=T[ta][:, 0:512], in_=P[b0][:], func=AF.Copy),
                      reads=[("P", b0)], writes=[("T", ta)])
                zreads = [("Z", n)] + ([("Z", n - 1)] if n > 0 else ["ZPhalo"])
                S.add("dve", lambda e, ta=ta, b0=b0, zc=zc: e.tensor_tensor(out=ZP[:, zc:zc + 512], in0=P[b0 + 1][:], in1=T[ta][:, 0:512], op=ALU.mult),
                      reads=[("P", b0 + 1), ("T", ta)], writes=[("Z", n)])
                S.add("dve", lambda e, tb=tb, zc=zc: e.tensor_scalar(out=T[tb][:, 0:512], in0=ZP[:, zc:zc + 512], scalar1=WC[lp][:, j, 2:3], scalar2=None, op0=ALU.mult),
                      reads=[("Z", n), ("WC", lp)], writes=[("T", tb)])
                S.add("dve", lambda e, tb=tb, tc=tc, zc=zc: e.scalar_tensor_tensor(out=T[tc][:, 0:512], in0=ZP[:, zc - 1:zc + 511], scalar=WC[lp][:, j, 1:2],
                                                                                 in1=T[tb][:, 0:512], op0=ALU.mult, op1=ALU.add),
                      reads=zreads + [("WC", lp), ("T", tb)], writes=[("T", tc)])
                S.add("dve", lambda e, tb=tb, tc=tc, zc=zc: e.scalar_tensor_tensor(out=T[tb][:, 0:512], in0=ZP[:, zc - 2:zc + 510], scalar=WC[lp][:, j, 0:1],
                                                                                 in1=T[tc][:, 0:512], op0=ALU.mult, op1=ALU.add),
                      reads=zreads + [("WC", lp), ("T", tc)], writes=[("T", tb)])

                def tail():
                    col_chunk_mm(l, 2 + 3 * j + 2, n, b0 + 2)
                    S.add("dve", lambda e, tb=tb, b0=b0, n=n: e.tensor_tensor(out=BIG[:, j, n * 512:(n + 1) * 512], in0=P[b0 + 2][:], in1=T[tb][:, 0:512], op=ALU.mult),
                          reads=[("P", b0 + 2), ("T", tb)], writes=[("BIG", j, 4 * n + t) for t in range(4)])
                return tail

        def pool_unit(l, c, n):
            lp = l % 2
            if True:
                bp = n % 2
                col_chunk_mm(l, c, n, 3)
                zc = 16 + n * 512
                S.add("act", lambda e, zc=zc: e.activation(out=ZP[:, zc:zc + 512], in_=P[3][:], func=AF.Copy),
                      reads=[("P", 3)], writes=[("Z", n)])
                zreads = [("Z", n)] + ([("Z", n - 1)] if n > 0 else ["ZPhalo"])
                lo = zc - 16
                ta, tb = next_T(), next_T()
                S.add("dve", lambda e, ta=ta, lo=lo: e.tensor_tensor(out=T[ta][:, 1:528], in0=ZP[:, lo + 1:lo + 528], in1=ZP[:, lo:lo + 527], op=ALU.add),
                      reads=zreads, writes=[("T", ta)])
                if c == 0:
                    cur, sh, st = ta, 2, 3
                    fin = tb
                else:
                    S.add("dve", lambda e, ta=ta, tb=tb: e.tensor_tensor(out=T[tb][:, 3:528], in0=T[ta][:, 3:528], in1=T[ta][:, 1:526], op=ALU.add),
                          reads=[("T", ta)], writes=[("T", tb)])
                    tcx = next_T()
                    S.add("dve", lambda e, tcx=tcx, tb=tb: e.tensor_tensor(out=T[tcx][:, 7:528], in0=T[tb][:, 7:528], in1=T[tb][:, 3:524], op=ALU.add),
                          reads=[("T", tb)], writes=[("T", tcx)])
                    cur, sh, st = tcx, 8, 15
                    fin = ta
                S.add("dve", lambda e, cur=cur, fin=fin, sh=sh, st=st: e.scalar_tensor_tensor(out=T[fin][:, st:528], in0=T[cur][:, st - sh:528 - sh], scalar=FLAG[:, 0:1],
                                                                                             in1=T[cur][:, st:528], op0=ALU.mult, op1=ALU.add),
                      reads=[("T", cur), "FLAG"], writes=[("T", fin)])
                S.add("dve", lambda e, fin=fin, bp=bp, zc=zc: e.scalar_tensor_tensor(out=DB[bp][:], in0=T[fin][:, 16:528], scalar=PINVW[:, c:c + 1],
                                                                                    in1=ZP[:, zc:zc + 512], op0=ALU.mult, op1=ALU.subtract),
                      reads=[("T", fin), "PINVW", ("Z", n)], writes=[("DB", bp)])
                if n == 0:
                    S.add("dve", lambda e, fin=fin, cur=cur: e.tensor_tensor(out=T[cur][:, 0:16], in0=T[fin][:, 16:32], in1=PINV[:, c, :], op=ALU.mult),
                          reads=[("T", fin), "PINV"], writes=[("T", cur)])
                    S.add("dve", lambda e, cur=cur, bp=bp, zc=zc: e.tensor_tensor(out=DB[bp][:, 0:16], in0=T[cur][:, 0:16], in1=ZP[:, zc:zc + 16], op=ALU.subtract),
                          reads=[("T", cur), ("Z", n), ("DB", bp)], writes=[("DB", bp)])

                def tail():
                    S.add("pe", lambda e, bp=bp: e.matmul(P[4][:], lhsT=WPBD[lp][:, c, :], rhs=DB[bp][:], start=True, stop=True),
                          reads=[("WPBD", lp), ("DB", bp)], writes=[("P", 4)])
                    S.add("act", lambda e, n=n: e.activation(out=BIG[:, 3 + c, n * 512:(n + 1) * 512], in_=P[4][:], func=AF.Identity, scale=PSC[lp][:, c:c + 1]),
                          reads=[("P", 4), ("PSC", lp)], writes=[("BIG", 3 + c, 4 * n + t) for t in range(4)])
                return tail

        def wo_tile(l, i):
            pl = plans[l]
            p = i % 2
            for nh in range(2):
                bank = 2 * p + nh
                for k in range(8):
                    slot, rk = wslot(pl["wo"][(k // 4) * 2 + nh])
                    wv = slot[:].rearrange("p (kc n) -> p kc n", n=512)
                    S.add("pe", lambda e, k=k, wv=wv, bank=bank: e.matmul(P[bank][:], lhsT=BIG[:, k, i * 128:(i + 1) * 128], rhs=wv[:, k % 4, :],
                                                                        start=(k == 0), stop=(k == 7)),
                          reads=[("BIG", k, i), rk], writes=[("P", bank)])
                S.add("dve", lambda e, nh=nh, bank=bank: e.scalar_tensor_tensor(out=X[:, i, nh * 512:(nh + 1) * 512], in0=X[:, i, nh * 512:(nh + 1) * 512], scalar=ALPHA,
                                                                              in1=P[bank][:], op0=ALU.mult, op1=ALU.add),
                      reads=[("X", i), ("P", bank)], writes=[("X", i)])

        def gate_up(l, m, ml, n, ctr):
            pl = plans[l]
            slot, rk = wslot(pl["gu"][m])
            wv = slot[:].rearrange("p (h kc n) -> p h kc n", h=2, n=128)
            b0 = 2 * (ctr % 2)
            for h in range(2):
                for k in range(8):
                    S.add("pe", lambda e, h=h, k=k: e.matmul(P[b0 + h][:], lhsT=wv[:, h, k, :], rhs=XT[:, k, n * 512:(n + 1) * 512], start=(k == 0), stop=(k == 7)),
                          reads=[("XT", 4 * n + t) for t in range(4)] + [rk], writes=[("P", b0 + h)])
            ta = next_T()
            S.add("act", lambda e: e.activation(out=T[ta][:, 0:512], in_=P[b0][:], func=AF.Silu), reads=[("P", b0)], writes=[("T", ta)])
            S.add("dve", lambda e: e.tensor_tensor(out=BIG[:, ml, n * 512:(n + 1) * 512], in0=P[b0 + 1][:], in1=T[ta][:, 0:512], op=ALU.mult),
                  reads=[("P", b0 + 1), ("T", ta)], writes=[("BIG", ml, 4 * n + t) for t in range(4)])

        def down_tile(l, si, i, first):
            pl = plans[l]
            m0, nm = SLABS[si]
            p = i % 2
            for nh in range(2):
                bank = 2 * p + nh
                for kk in range(nm):
                    slot, rk = wslot(pl["dn"][si][kk // 2])
                    wv = slot[:].rearrange("p (kc n) -> p kc n", n=1024)
                    S.add("pe", lambda e, kk=kk, wv=wv, bank=bank, nh=nh: e.matmul(P[bank][:], lhsT=BIG[:, kk, i * 128:(i + 1) * 128],
                                                                                 rhs=wv[:, kk % 2, nh * 512:(nh + 1) * 512], start=(kk == 0), stop=(kk == nm - 1)),
                          reads=[("BIG", kk, i), rk], writes=[("P", bank)])
                if first:
                    S.add("dve", lambda e, nh=nh, bank=bank: e.scalar_tensor_tensor(out=X[:, i, nh * 512:(nh + 1) * 512], in0=X[:, i, nh * 512:(nh + 1) * 512], scalar=ALPHA,
                                                                                  in1=P[bank][:], op0=ALU.mult, op1=ALU.add),
                          reads=[("X", i), ("P", bank)], writes=[("X", i)])
                else:
                    S.add("dve", lambda e, nh=nh, bank=bank: e.tensor_tensor(out=X[:, i, nh * 512:(nh + 1) * 512], in0=X[:, i, nh * 512:(nh + 1) * 512],
                                                                           in1=P[bank][:], op=ALU.add),
                          reads=[("X", i), ("P", bank)], writes=[("X", i)])

        outs = []
        load_small(layers[0])
        for i in range(NT):
            cast_transpose(i)
        for li, l in enumerate(layers):
            pl = plans[l]
            load_gb(l, 1)
            cv = pl["cv"]
            units = []
            for c in range(2):
                for n in range(NG):
                    rel = [cv[0]] if (c == 1 and n == NG - 1) else None
                    units.append((lambda c=c, n=n: pool_unit(l, c, n), rel))
            for j in range(3):
                for n in range(NG):
                    rel = {0: [cv[1]], 1: [cv[2], cv[3]], 2: [cv[4], cv[5]]}[j] if n == NG - 1 else None
                    units.append((lambda j=j, n=n: conv_unit(l, j, n), rel))
            ui = 0
            pending = [None]
            nit = NT + 3
            for t in range(nit):
                def sgu_part(t=t):
                    if t < NT:
                        sgu_a(l, t)
                        if t == NT - 1:
                            ring_release(pl["uv"])
                    if 0 <= t - 2 < NT:
                        sgu_c1(l, t - 2)
                    if 0 <= t - 1 < NT:
                        sgu_b(l, t - 1)

                def sgu_part2(t=t):
                    if 0 <= t - 2 < NT:
                        sgu_mix(l, t - 2)
                        sgu_c2(l, t - 2)
                    if 0 <= t - 3 < NT:
                        sgu_tr(l, t - 3)

                def unit_part(t=t):
                    nonlocal ui
                    target = min(len(units), -(-(t + 1) * len(units) // nit))
                    while ui < target:
                        fn, rel = units[ui]
                        if pending[0] is not None:
                            ptail, prel = pending[0]
                            ptail()
                            if prel:
                                ring_release(prel)
                        pending[0] = (fn(), rel)
                        ui += 1
                    if t == nit - 1 and pending[0] is not None:
                        ptail, prel = pending[0]
                        ptail()
                        if prel:
                            ring_release(prel)
                        pending[0] = None

                sgu_part()
                unit_part()
                sgu_part2()
            if dump is not None and dump[0] == "A":
                break
            gu_done = set()
            gu_left = {m: NG for m in range(NFF)}
            cnt["gu"] = 0

            def do_gate_up(m, ml, n):
                gate_up(l, m, ml, n, cnt["gu"])
                cnt["gu"] += 1
                gu_done.add((m, n))
                gu_left[m] -= 1
                if gu_left[m] == 0:
                    ring_release([pl["gu"][m]])

            for t in range(NT + 6):
                if 0 <= t - 2 < NT:
                    ln_norm(t - 2)
                if 0 <= t - 3 < NT:
                    ln_bias(t - 3)
                if 0 <= t - 5 < NT:
                    cast_stage(t - 5)
                if 0 <= t - 6 < NT:
                    transpose_stage(t - 6)
                if t < NT:
                    wo_tile(l, t)
                    ln_stats(t)
                if NT <= t < NT + 4:
                    ring_release([pl["wo"][t - NT]])
                if t >= NT and not (dump is not None and dump[0] == "B"):
                    emitted = 0
                    for m in range(SLABS[0][1]):
                        for n in range(NG):
                            if emitted >= 3 or (m, n) in gu_done:
                                continue
                            if (t - 6) < 4 * n + 3 or pl["gu"][m] not in ring_state["slot_of"]:
                                continue
                            do_gate_up(m, m, n)
                            emitted += 1
            if dump is not None and dump[0] == "B":
                break
            load_gb(l, 2)
            if li + 1 < len(layers):
                load_small(layers[li + 1])
            for si, (m0, nm) in enumerate(SLABS):
                for ml in range(nm):
                    m = m0 + ml
                    for n in range(NG):
                        if (m, n) not in gu_done:
                            do_gate_up(m, ml, n)
                last = si == len(SLABS) - 1
                if not last:
                    for i in range(NT):
                        down_tile(l, si, i, si == 0)
                else:
                    more = li + 1 < len(layers)
                    for t in range(NT + 6):
                        if 0 <= t - 2 < NT:
                            ln_norm(t - 2)
                        if 0 <= t - 3 < NT:
                            ln_bias(t - 3)
                            if not more:
                                i = t - 3
                                outs.append(S.add("sp", lambda e, i=i: e.dma_start(out=out_d[i * 128:(i + 1) * 128, :], in_=X[:, i, :]),
                                                  reads=[("X", i)], dma=True))
                        if more and 0 <= t - 5 < NT:
                            cast_stage(t - 5)
                        if more and 0 <= t - 6 < NT:
                            transpose_stage(t - 6)
                        if t < NT:
                            down_tile(l, si, t, si == 0)
                            ln_stats(t)
                ring_release(pl["dn"][si])
        if dump is not None:
            kind = dump[3]
            if kind == "BIG":
                outs.append(S.add("sp", lambda e: e.dma_start(out=dump_d, in_=BIG[:]),
                                  reads=[("BIG", c, i) for c in range(8) for i in range(NT)], dma=True))
            elif kind == "X":
                for i in range(NT):
                    outs.append(S.add("sp", lambda e, i=i: e.dma_start(out=dump_d[:, i, :], in_=X[:, i, :]), reads=[("X", i)], dma=True))
            elif kind == "XT":
                outs.append(S.add("sp", lambda e: e.dma_start(out=dump_d, in_=XT[:]), reads=[("XT", i) for i in range(NT)], dma=True))
        S.emit(nc, sems, dsems, outs)
    return nc


def _host_inputs(inputs, b):
    f = lambda a: np.ascontiguousarray(np.asarray(a, dtype=np.float32))
    m = {
        "x": f(inputs["x"][b]),
        "w_in": f(inputs["w_in"]),
        "w_convT": f(np.asarray(inputs["w_conv"]).reshape(DEPTH, 3, 3, 128).transpose(0, 3, 2, 1)),
        "w_pool": f(inputs["w_pool"]),
        "pool_scaleT": f(np.asarray(inputs["pool_scale"]).reshape(DEPTH, 2, 128).transpose(0, 2, 1)),
        "sgu_ln_g": f(inputs["sgu_ln_g"]),
        "w_spatialT": f(np.asarray(inputs["w_spatial"]).transpose(0, 3, 1, 2)),
        "b_spatialT": f(np.asarray(inputs["b_spatial"]).transpose(0, 2, 1)),
        "w_o": f(inputs["w_o"]),
        "ln1_g": f(inputs["ln1_g"]), "ln1_b": f(inputs["ln1_b"]),
        "ln2_g": f(inputs["ln2_g"]), "ln2_b": f(inputs["ln2_b"]),
        "w_gate_up": f(inputs["w_gate_up"]),
        "w_down": f(inputs["w_down"]),
    }
    m.update(_const_inputs())
    return m


_NC_CACHE = {}


def kernel(**inputs):
    key = "full"
    if key not in _NC_CACHE:
        _NC_CACHE[key] = build_program(list(range(DEPTH)))
    nc = _NC_CACHE[key]
    in_maps = [_host_inputs(inputs, b) for b in range(8)]
    res = run_bass_kernel_spmd(nc, in_maps, core_ids=list(range(8)))
    out = np.stack([np.asarray(r["out"], dtype=np.float32) for r in res.results], axis=0)
    return out
```

```python
import numpy as np
import concourse.bass as bass
import concourse.mybir as mybir
from concourse.bass_utils import run_bass_kernel_spmd

F32 = mybir.dt.float32
BF16 = mybir.dt.bfloat16
AF = mybir.ActivationFunctionType
ALU = mybir.AluOpType

COMPUTE = ("pe", "act", "dve", "pool")


class Op:
    __slots__ = ("eng", "fn", "dma", "deps", "milestone", "count", "sem", "idx", "name")

    def __init__(self, eng, fn, dma, name):
        self.eng = eng
        self.fn = fn
        self.dma = dma
        self.deps = []
        self.milestone = False
        self.count = 0
        self.sem = None
        self.idx = 0
        self.name = name


class Sched:
    def __init__(self, n_dma_sems=12):
        self.streams = {e: [] for e in ("pe", "act", "dve", "pool", "sp")}
        self.last_writer = {}
        self.readers = {}
        self.n_dma_sems = n_dma_sems
        self.dma_rr = {}
        self.dma_last = {}
        self.nops = 0
        self._cap = None

    def capture(self, fn):
        assert self._cap is None
        self._cap = []
        fn()
        ops, self._cap = self._cap, None
        return ops

    def replay_merged(self, a, b):
        ia = ib = 0
        while ia < len(a) or ib < len(b):
            if ib >= len(b) or (ia < len(a) and ia * len(b) <= ib * len(a)):
                self.add(*a[ia])
                ia += 1
            else:
                self.add(*b[ib])
                ib += 1

    def add(self, eng, fn, reads=(), writes=(), dma=False, name=""):
        if self._cap is not None:
            self._cap.append((eng, fn, tuple(reads), tuple(writes), dma, name))
            return None
        op = Op(eng, fn, dma, name)
        op.idx = self.nops
        self.nops += 1
        deps = {}

        def consider(d, raw):
            if d is None or d is op:
                return
            if not d.dma and not op.dma and d.eng == op.eng:
                if op.eng == "pe":
                    return
                if not raw:
                    return
            deps[id(d)] = d

        for r in reads:
            consider(self.last_writer.get(r), True)
        for r in writes:
            consider(self.last_writer.get(r), False)
            for rd in self.readers.get(r, ()):
                consider(rd, False)
        if dma:
            q = self.dma_rr.get(eng, 0)
            self.dma_rr[eng] = q + 1
            slot = (eng, q % self.n_dma_sems)
            prev = self.dma_last.get(slot)
            op.sem = slot
            op.count = (prev.count if prev is not None else 0) + 16
            if prev is not None:
                deps[id(prev)] = prev
            self.dma_last[slot] = op
        for d in deps.values():
            if not d.dma:
                d.milestone = True
        op.deps = list(deps.values())
        for r in writes:
            self.last_writer[r] = op
            self.readers[r] = []
        for r in reads:
            self.readers.setdefault(r, []).append(op)
        self.streams[eng].append(op)
        return op

    def emit(self, nc, sems, dma_sems, final_waits):
        for e in COMPUTE:
            c = 0
            for op in self.streams[e]:
                if not op.dma and op.milestone:
                    c += 1
                    op.count = c
        engs = {"pe": "tensor", "act": "scalar", "dve": "vector", "pool": "gpsimd", "sp": "sync"}

        def run(ename, eng):
            waited = {}
            for op in self.streams[ename]:
                for d in op.deps:
                    s = dma_sems[d.sem] if d.dma else sems[d.eng]
                    key = d.sem if d.dma else d.eng
                    if waited.get(key, 0) < d.count:
                        eng.wait_ge(s, d.count)
                        waited[key] = d.count
                inst = op.fn(eng)
                if op.dma:
                    inst.then_inc(dma_sems[op.sem], 16)
                elif op.milestone:
                    inst.then_inc(sems[ename], 1)
            if ename == "sp":
                for d in final_waits:
                    eng.wait_ge(dma_sems[d.sem], d.count)

        with nc.Block() as block:
            for ename, attr in engs.items():
                if not self.streams[ename] and ename != "sp":
                    continue
                getattr(block, attr)(lambda eng, ename=ename: run(ename, eng))


DEPTH = 4
D = 1024
SEQ = 2048
NT = 16
NG = 4
C0 = 384
PW = 256
SW = 384
INW = 2176
DFF = 2816
NFF = DFF // 128
ALPHA = float((2 * DEPTH) ** 0.25)
EPS = 1e-5
POOL_WINDOWS = (2, 4, 8, 16)
RSLOTS = 6
SLABS = ((0, 6), (6, 8), (14, 8))
AX = mybir.AxisListType.X


def _const_inputs():
    ident = np.eye(128, dtype=np.float32)
    maskT = np.triu(np.ones((128, 128), dtype=np.float32))
    pinv = np.zeros((128, 2, 16), dtype=np.float32)
    pinvw = np.zeros((128, 2), dtype=np.float32)
    for c in range(2):
        for p in range(128):
            w = POOL_WINDOWS[2 * c + p // 64]
            pinvw[p, c] = 1.0 / w
            for t in range(16):
                pinv[p, c, t] = 1.0 / min(t + 1, w)
    flag = np.zeros((128, 1), dtype=np.float32)
    flag[64:] = 1.0
    return {"c_ident": ident, "c_maskT": maskT, "c_pinv": pinv, "c_pinvw": pinvw, "c_flag": flag}


def build_program(layers, dump=None):
    from contextlib import ExitStack

    nc = bass.Bass("TRN2", target_bir_lowering=False)
    dr = lambda name, shape, dt=F32, kind="ExternalInput": nc.dram_tensor(name, shape, dt, kind=kind)
    x_d = dr("x", [SEQ, D]).ap()
    w_in_d = dr("w_in", [DEPTH, D, INW]).ap()
    w_convT_d = dr("w_convT", [DEPTH, 128, 3, 3]).ap()
    w_pool_d = dr("w_pool", [DEPTH, 4, 64, 64]).ap()
    pscale_d = dr("pool_scaleT", [DEPTH, 128, 2]).ap()
    sgu_g_h = dr("sgu_ln_g", [DEPTH, SW])
    w_spT_d = dr("w_spatialT", [DEPTH, 128, 6, 128]).ap()
    b_spT_d = dr("b_spatialT", [DEPTH, 128, 6]).ap()
    w_o_d = dr("w_o", [DEPTH, D, D]).ap()
    ln_h = {k: dr(k, [DEPTH, D]) for k in ("ln1_g", "ln1_b", "ln2_g", "ln2_b")}
    w_gu_d = dr("w_gate_up", [DEPTH, D, 2 * DFF]).ap()
    w_dn_d = dr("w_down", [DEPTH, DFF, D]).ap()
    c_ident_d = dr("c_ident", [128, 128]).ap()
    c_maskT_d = dr("c_maskT", [128, 128]).ap()
    c_pinv_d = dr("c_pinv", [128, 2, 16]).ap()
    c_pinvw_d = dr("c_pinvw", [128, 2]).ap()
    c_flag_d = dr("c_flag", [128, 1]).ap()
    out_d = dr("out", [SEQ, D], F32, "ExternalOutput").ap()
    dump_d = None
    if dump is not None:
        dump_d = dr("dump", dump[1], dump[2], "ExternalOutput").ap()

    S = Sched()
    with ExitStack() as es:
        sb = lambda n, s, d=F32: es.enter_context(nc.sbuf_tensor(n, s, d))
        ps = lambda n, s, d=F32: es.enter_context(nc.psum_tensor(n, s, d))
        X = sb("X", [128, NT, D])
        XB = [sb(f"XB{p}", [128, D], BF16) for p in range(2)]
        XT = sb("XT", [128, 8, SEQ], BF16)
        BIG = sb("BIG", [128, 8, SEQ], BF16)
        RING = [sb(f"W{r}", [128, 2048], BF16) for r in range(RSLOTS)]
        GBUF = sb("GBUF", [128, D])
        BBUF = sb("BBUF", [128, D])
        ident_f = sb("ident_f", [128, 128])
        ident_b = sb("ident_b", [128, 128], BF16)
        maskT = sb("maskT", [128, 128])
        WMT = sb("WMT", [128, 6, 128], BF16)
        BSP = [sb(f"BSP{p}", [128, 6]) for p in range(2)]
        GSG = sb("GSG", [128, SW])
        WC = [sb(f"WC{p}", [128, 3, 3]) for p in range(2)]
        PSC = [sb(f"PSC{p}", [128, 2]) for p in range(2)]
        WPBD = [sb(f"WPBD{p}", [128, 2, 128], BF16) for p in range(2)]
        PINV = sb("PINV", [128, 2, 16])
        PINVW = sb("PINVW", [128, 2])
        FLAG = sb("FLAG", [128, 1])
        CEPS = sb("CEPS", [128, 8])
        CNH = sb("CNH", [128, 8])
        U = [sb(f"U{p}", [128, SW]) for p in range(3)]
        V = [sb(f"V{p}", [128, SW]) for p in range(3)]
        SQ = [sb(f"SQ{p}", [128, SW]) for p in range(3)]
        VN = [sb(f"VN{p}", [128, SW], BF16) for p in range(2)]
        YC = [sb(f"YC{p}", [128, SW], BF16) for p in range(2)]
        SMALL = [sb(f"SMALL{p}", [128, 8, 6]) for p in range(3)]
        ZP = sb("ZP", [128, 16 + SEQ])
        NTT = 4
        T = [sb(f"T{k}", [128, 528]) for k in range(NTT)]
        DB = [sb(f"DB{p}", [128, 512], BF16) for p in range(2)]
        ST = [sb(f"ST{p}", [128, 2, 6]) for p in range(4)]
        MV = [sb(f"MV{p}", [128, 8]) for p in range(4)]
        CNEG = sb("CNEG", [128, 1])
        P = [ps(f"P{b}", [128, 512]) for b in range(6)]
        PT = [ps(f"PT{b}", [128, 8, 128], BF16) for b in range(2)]

        sems = {e: es.enter_context(nc.semaphore("s_" + e)) for e in COMPUTE}
        dsems = {}
        for q in ("sp", "pool"):
            for k in range(S.n_dma_sems):
                dsems[(q, k)] = es.enter_context(nc.semaphore(f"d_{q}{k}"))

        cnt = {"pt": 0, "t": 0}

        def next_T():
            k = cnt["t"] % NTT
            cnt["t"] += 1
            return k

        tiles = []

        def cols_src(w2d, c, n=128):
            return w2d[:, c:c + n].rearrange("(kc p) n -> p kc n", p=128)

        def plan_layer(l):
            idx = {}
            win = w_in_d[l]
            idx["uv"] = []
            for q in range(3):
                idx["uv"].append(len(tiles))
                tiles.append([(lambda s: s[:].rearrange("p (kc n) -> p kc n", n=256), cols_src(win, 3 * C0 + PW + q * 256, 256))])
            order = [3 * C0, 3 * C0 + 128]
            for j in range(3):
                order += [j * 128, 2 * C0 + j * 128, C0 + j * 128]
            idx["cv"] = []
            for q in range(0, len(order), 2):
                pair = order[q:q + 2]
                idx["cv"].append(len(tiles))
                tiles.append([((lambda s, h=h: s[:].rearrange("p (h kc n) -> p h kc n", h=2, n=128)[:, h]), cols_src(win, c))
                              for h, c in enumerate(pair)])
            idx["wo"] = []
            for kh in range(2):
                for nh in range(2):
                    idx["wo"].append(len(tiles))
                    src = w_o_d[l][kh * 512:(kh + 1) * 512, nh * 512:(nh + 1) * 512].rearrange("(kc p) n -> p kc n", p=128)
                    tiles.append([(lambda s: s[:].rearrange("p (kc n) -> p kc n", n=512), src)])
            idx["gu"] = {}
            idx["dn"] = {}
            for si, (m0, nm) in enumerate(SLABS):
                for m in range(m0, m0 + nm):
                    idx["gu"][m] = len(tiles)
                    tiles.append([((lambda s, h=h: s[:].rearrange("p (h kc n) -> p h kc n", h=2, n=128)[:, h]), cols_src(w_gu_d[l], c))
                                  for h, c in enumerate((m * 128, DFF + m * 128))])
                idx["dn"][si] = []
                for kk in range(0, nm, 2):
                    idx["dn"][si].append(len(tiles))
                    src = w_dn_d[l][(m0 + kk) * 128:(m0 + kk + 2) * 128, :].rearrange("(kc p) n -> p kc n", p=128)
                    tiles.append([(lambda s: s[:].rearrange("p (kc n) -> p kc n", n=1024), src)])
            return idx

        plans = {l: plan_layer(l) for l in layers}
        ring_state = {"next": 0, "free": list(range(RSLOTS)), "slot_of": {}}

        def ring_pump():
            while ring_state["free"] and ring_state["next"] < len(tiles):
                t = ring_state["next"]
                sl = ring_state["free"].pop(0)
                ring_state["slot_of"][t] = sl
                slot = RING[sl]
                for dstf, src in tiles[t]:
                    S.add("pool", lambda e, dstf=dstf, src=src, slot=slot: e.dma_start(out=dstf(slot), in_=src),
                          writes=[("W", sl)], dma=True, name=f"wload{t}")
                ring_state["next"] += 1

        def ring_release(ts):
            for t in ts:
                ring_state["free"].append(ring_state["slot_of"].pop(t))
            ring_pump()

        def wslot(t):
            sl = ring_state["slot_of"][t]
            return RING[sl], ("W", sl)

        def bcast_row(handle, l, n):
            return bass.AP(handle, l * n, [[0, 128], [1, n]])

        S.add("sp", lambda e: e.dma_start(out=ident_f[:], in_=c_ident_d), writes=["ident_f"], dma=True)
        S.add("sp", lambda e: e.dma_start(out=maskT[:], in_=c_maskT_d), writes=["maskT"], dma=True)
        for i in range(NT):
            S.add("sp", lambda e, i=i: e.dma_start(out=X[:, i, :], in_=x_d[i * 128:(i + 1) * 128, :]), writes=[("X", i)], dma=True)
        S.add("sp", lambda e: e.dma_start(out=PINV[:], in_=c_pinv_d), writes=["PINV"], dma=True)
        S.add("sp", lambda e: e.dma_start(out=PINVW[:], in_=c_pinvw_d), writes=["PINVW"], dma=True)
        S.add("sp", lambda e: e.dma_start(out=FLAG[:], in_=c_flag_d), writes=["FLAG"], dma=True)
        S.add("dve", lambda e: e.tensor_copy(out=ident_b[:], in_=ident_f[:]), reads=["ident_f"], writes=["ident_b"])
        S.add("pool", lambda e: e.memset(CEPS[:], EPS), writes=["CEPS"])
        S.add("pool", lambda e: e.memset(CNH[:], -0.5), writes=["CNH"])
        S.add("pool", lambda e: e.memset(CNEG[:], -1.0), writes=["CNEG"])
        S.add("pool", lambda e: e.memset(ZP[:, 0:16], 0.0), writes=["ZPhalo"])
        for p in range(2):
            S.add("pool", lambda e, p=p: e.memset(WPBD[p][:], 0.0), writes=[("WPBD", p)])
        ring_pump()

        def load_small(l):
            p = l % 2
            for k in range(2):
                uv_ = U[k][:].rearrange("p (h t) -> p h t", t=128)
                S.add("sp", lambda e, k=k, uv_=uv_: e.dma_start(out=uv_, in_=w_spT_d[l][:, 3 * k:3 * k + 3, :]), writes=[("U", k)], dma=True)
                S.add("dve", lambda e, k=k, uv_=uv_: e.tensor_tensor(out=WMT[:, 3 * k:3 * k + 3, :], in0=uv_, in1=maskT[:].unsqueeze(1).to_broadcast([128, 3, 128]), op=ALU.mult),
                      reads=[("U", k), "maskT"], writes=["WMT"])
            S.add("sp", lambda e: e.dma_start(out=BSP[p][:], in_=b_spT_d[l]), writes=[("BSP", p)], dma=True)
            S.add("sp", lambda e: e.dma_start(out=GSG[:], in_=bcast_row(sgu_g_h, l, SW)), writes=["GSG"], dma=True)
            S.add("sp", lambda e: e.dma_start(out=WC[p][:], in_=w_convT_d[l]), writes=[("WC", p)], dma=True)
            S.add("sp", lambda e: e.dma_start(out=PSC[p][:], in_=pscale_d[l]), writes=[("PSC", p)], dma=True)
            for c in range(2):
                for h in range(2):
                    S.add("pool", lambda e, c=c, h=h: e.dma_start(out=WPBD[p][h * 64:(h + 1) * 64, c, h * 64:(h + 1) * 64], in_=w_pool_d[l, 2 * c + h]),
                          writes=[("WPBD", p)], dma=True)

        def load_gb(l, which):
            S.add("sp", lambda e: e.dma_start(out=GBUF[:], in_=bcast_row(ln_h[f"ln{which}_g"], l, D)), writes=["GBUF"], dma=True)
            S.add("sp", lambda e: e.dma_start(out=BBUF[:], in_=bcast_row(ln_h[f"ln{which}_b"], l, D)), writes=["BBUF"], dma=True)

        def cast_stage(i):
            q = i % 2
            S.add("act", lambda e: e.activation(out=XB[q][:], in_=X[:, i, :], func=AF.Copy), reads=[("X", i)], writes=[("XB", q)])

        def transpose_stage(i):
            q = i % 2
            b = cnt["pt"] % 2
            cnt["pt"] += 1
            for k in range(8):
                S.add("pe", lambda e, k=k: e.transpose(PT[b][:, k, :], XB[q][:, k * 128:(k + 1) * 128], ident_b[:]),
                      reads=[("XB", q), "ident_b"], writes=[("PT", b)])
            S.add("act", lambda e: e.activation(out=XT[:, :, i * 128:(i + 1) * 128], in_=PT[b][:], func=AF.Copy),
                  reads=[("PT", b)], writes=[("XT", i)])

        def cast_transpose(i):
            cast_stage(i)
            transpose_stage(i)

        def ln_stats(i):
            p = i % 4
            for c in range(2):
                S.add("dve", lambda e, c=c: e.bn_stats(out=ST[p][:, c, :], in_=X[:, i, c * 512:(c + 1) * 512]),
                      reads=[("X", i)], writes=[("ST", p)])
            S.add("dve", lambda e: e.bn_aggr(out=MV[p][:, 0:2], in_=ST[p][:].rearrange("p a b -> p (a b)")),
                  reads=[("ST", p)], writes=[("MV", p)])
            S.add("pool", lambda e: e.tensor_tensor(out=MV[p][:, 2:3], in0=MV[p][:, 1:2], in1=CEPS[:, 0:1], op=ALU.add),
                  reads=[("MV", p), "CEPS"], writes=[("MVe", p)])
            S.add("pool", lambda e: e.tensor_tensor(out=MV[p][:, 3:4], in0=MV[p][:, 2:3], in1=CNH[:, 0:1], op=ALU.pow),
                  reads=[("MVe", p), "CNH"], writes=[("MVr", p)])
            S.add("pool", lambda e: e.tensor_tensor(out=MV[p][:, 4:5], in0=MV[p][:, 0:1], in1=MV[p][:, 3:4], op=ALU.mult),
                  reads=[("MV", p), ("MVr", p)], writes=[("MVm", p)])
            S.add("pool", lambda e: e.tensor_tensor(out=MV[p][:, 5:6], in0=MV[p][:, 4:5], in1=CNEG[:, 0:1], op=ALU.mult),
                  reads=[("MVm", p), "CNEG"], writes=[("MVn", p)])

        def ln_norm(i):
            p = i % 4
            S.add("act", lambda e: e.activation(out=X[:, i, :], in_=X[:, i, :], func=AF.Identity, scale=MV[p][:, 3:4], bias=MV[p][:, 5:6]),
                  reads=[("X", i), ("MVr", p), ("MVn", p)], writes=[("X", i)])
            S.add("pool", lambda e: e.tensor_tensor(out=X[:, i, :], in0=X[:, i, :], in1=GBUF[:], op=ALU.mult),
                  reads=[("X", i), "GBUF"], writes=[("X", i)])

        def ln_bias(i):
            S.add("dve", lambda e: e.tensor_tensor(out=X[:, i, :], in0=X[:, i, :], in1=BBUF[:], op=ALU.add),
                  reads=[("X", i), "BBUF"], writes=[("X", i)])

        v3 = lambda t: t[:].rearrange("p (h d) -> p h d", d=64)

        def sgu_a(l, i):
            pl = plans[l]
            p = i % 3
            A, Bk = P[0], P[1]
            (s0, r0), (s1, r1), (s2, r2) = [wslot(t) for t in pl["uv"]]
            v0 = s0[:].rearrange("p (kc n) -> p kc n", n=256)
            v1 = s1[:].rearrange("p (kc n) -> p kc n", n=256)
            v2 = s2[:].rearrange("p (kc n) -> p kc n", n=256)
            groups = [(A[:, 0:256], v0, 0, 256, r0, 0), (A[:, 256:384], v1, 0, 128, r1, 0),
                      (Bk[:, 0:128], v1, 128, 256, r1, 1), (Bk[:, 128:384], v2, 0, 256, r2, 1)]
            for (o, wv, ca, cb, rk, bank) in groups:
                for k in range(8):
                    S.add("pe", lambda e, o=o, wv=wv, ca=ca, cb=cb, k=k: e.matmul(o, lhsT=XT[:, k, i * 128:(i + 1) * 128], rhs=wv[:, k, ca:cb],
                                                                                 start=(k == 0), stop=(k == 7)),
                          reads=[("XT", i), rk], writes=[("P", bank)])
            S.add("act", lambda e: e.activation(out=U[p][:], in_=A[:, 0:SW], func=AF.Gelu), reads=[("P", 0)], writes=[("U", p)])
            S.add("act", lambda e: e.activation(out=V[p][:], in_=Bk[:, 0:SW], func=AF.Gelu), reads=[("P", 1)], writes=[("V", p)])
            S.add("act", lambda e: e.activation(out=SQ[p][:], in_=V[p][:], func=AF.Square), reads=[("V", p)], writes=[("SQ", p)])

        def sgu_b(l, i):
            p = i % 3
            sm = SMALL[p]
            S.add("dve", lambda e: e.tensor_reduce(out=sm[:, 0, :], in_=v3(V[p]), axis=AX, op=ALU.add), reads=[("V", p)], writes=[("sm0", p)])
            S.add("dve", lambda e: e.tensor_reduce(out=sm[:, 1, :], in_=v3(SQ[p]), axis=AX, op=ALU.add), reads=[("SQ", p)], writes=[("sm1", p)])
            S.add("dve", lambda e: e.tensor_scalar(out=sm[:, 2, :], in0=sm[:, 0, :], scalar1=1.0 / 64, scalar2=None, op0=ALU.mult),
                  reads=[("sm0", p)], writes=[("sm2", p)])
            S.add("dve", lambda e: e.tensor_tensor(out=sm[:, 3, :], in0=sm[:, 2, :], in1=sm[:, 2, :], op=ALU.mult),
                  reads=[("sm2", p)], writes=[("sm3", p)])
            S.add("dve", lambda e: e.scalar_tensor_tensor(out=sm[:, 4, :], in0=sm[:, 1, :], scalar=1.0 / 64, in1=sm[:, 3, :],
                                                          op0=ALU.mult, op1=ALU.subtract),
                  reads=[("sm1", p), ("sm3", p)], writes=[("sm4", p)])
            S.add("pool", lambda e: e.tensor_tensor(out=sm[:, 5, :], in0=sm[:, 4, :], in1=CEPS[:, 0:6], op=ALU.add),
                  reads=[("sm4", p), "CEPS"], writes=[("sm5", p)])
            S.add("pool", lambda e: e.tensor_tensor(out=sm[:, 6, :], in0=sm[:, 5, :], in1=CNH[:, 0:6], op=ALU.pow),
                  reads=[("sm5", p), "CNH"], writes=[("sm6", p)])

        def sgu_c1(l, i):
            p = i % 3
            q = i % 2
            sm = SMALL[p]
            S.add("dve", lambda e: e.tensor_tensor(out=v3(SQ[p]), in0=v3(V[p]), in1=sm[:, 2, :].unsqueeze(2).to_broadcast([128, 6, 64]), op=ALU.subtract),
                  reads=[("V", p), ("sm2", p), ("SQ", p)], writes=[("SQ", p)])
            S.add("dve", lambda e: e.tensor_tensor(out=v3(VN[q]), in0=v3(SQ[p]), in1=sm[:, 6, :].unsqueeze(2).to_broadcast([128, 6, 64]), op=ALU.mult),
                  reads=[("SQ", p), ("sm6", p)], writes=[("VN", q)])

        def sgu_mix(l, i):
            q = i % 2
            M = P[2]
            for h in range(6):
                S.add("pe", lambda e, h=h: e.matmul(M[:, h * 64:(h + 1) * 64], lhsT=WMT[:, h, :], rhs=VN[q][:, h * 64:(h + 1) * 64], start=True, stop=True),
                      reads=["WMT", ("VN", q)], writes=[("P", 2)])

        def sgu_c2(l, i):
            p = i % 3
            q = i % 2
            lp = l % 2
            M = P[2]
            S.add("dve", lambda e: e.tensor_tensor(out=SQ[p][:], in0=M[:, 0:SW], in1=GSG[:], op=ALU.mult),
                  reads=[("P", 2), "GSG", ("SQ", p)], writes=[("SQ", p)])
            S.add("dve", lambda e: e.tensor_tensor(out=v3(SQ[p]), in0=v3(SQ[p]), in1=BSP[lp][:].unsqueeze(2).to_broadcast([128, 6, 64]), op=ALU.add),
                  reads=[("SQ", p), ("BSP", lp)], writes=[("SQ", p)])
            S.add("dve", lambda e: e.tensor_tensor(out=YC[q][:], in0=SQ[p][:], in1=U[p][:], op=ALU.mult),
                  reads=[("SQ", p), ("U", p)], writes=[("YC", q)])

        def sgu_tr(l, i):
            q = i % 2
            b = cnt["pt"] % 2
            cnt["pt"] += 1
            for c in range(3):
                S.add("pe", lambda e, c=c: e.transpose(PT[b][:, c, :], YC[q][:, c * 128:(c + 1) * 128], ident_b[:]),
                      reads=[("YC", q), "ident_b"], writes=[("PT", b)])
            S.add("act", lambda e: e.activation(out=BIG[:, 5:8, i * 128:(i + 1) * 128], in_=PT[b][:, 0:3, :], func=AF.Copy),
                  reads=[("PT", b)], writes=[("BIG", 5, i), ("BIG", 6, i), ("BIG", 7, i)])

        def col_chunk_mm(l, q, n, bank):
            pl = plans[l]
            slot, rk = wslot(pl["cv"][q // 2])
            wv = slot[:].rearrange("p (h kc n) -> p h kc n", h=2, n=128)[:, q % 2]
            for k in range(8):
                S.add("pe", lambda e, k=k: e.matmul(P[bank][:], lhsT=wv[:, k, :], rhs=XT[:, k, n * 512:(n + 1) * 512], start=(k == 0), stop=(k == 7)),
                      reads=[("XT", 4 * n + t) for t in range(4)] + [rk], writes=[("P", bank)])

        def conv_unit(l, j, n):
            lp = l % 2
            if True:
                b0 = 3
                col_chunk_mm(l, 2 + 3 * j + 0, n, b0)
                col_chunk_mm(l, 2 + 3 * j + 1, n, b0 + 1)
                ta, tb, tc = next_T(), next_T(), next_T()
                zc = 16 + n * 512
                S.add("act", lambda e, ta=ta, b0=b0: e.activation(out=T[ta][:, 0:512], in_=P[b0][:], func=AF.Copy),
                      reads=[("P", b0)], writes=[("T", ta)])
                zreads = [("Z", n)] + ([("Z", n - 1)] if n > 0 else ["ZPhalo"])
                S.add("dve", lambda e, ta=ta, b0=b0, zc=zc: e.tensor_tensor(out=ZP[:, zc:zc + 512], in0=P[b0 + 1][:], in1=T[ta][:, 0:512], op=ALU.mult),
                      reads=[("P", b0 + 1), ("T", ta)], writes=[("Z", n)])
                S.add("dve", lambda e, tb=tb, zc=zc: e.tensor_scalar(out=T[tb][:, 0:512], in0=ZP[:, zc:zc + 512], scalar1=WC[lp][:, j, 2:3], scalar2=None, op0=ALU.mult),
                      reads=[("Z", n), ("WC", lp)], writes=[("T", tb)])
                S.add("dve", lambda e, tb=tb, tc=tc, zc=zc: e.scalar_tensor_tensor(out=T[tc][:, 0:512], in0=ZP[:, zc - 1:zc + 511], scalar=WC[lp][:, j, 1:2],
                                                                                 in1=T[tb][:, 0:512], op0=ALU.mult, op1=ALU.add),
                      reads=zreads + [("WC", lp), ("T", tb)], writes=[("T", tc)])
                S.add("dve", lambda e, tb=tb, tc=tc, zc=zc: e.scalar_tensor_tensor(out=T[tb][:, 0:512], in0=ZP[:, zc - 2:zc + 510], scalar=WC[lp][:, j, 0:1],
                                                                                 in1=T[tc][:, 0:512], op0=ALU.mult, op1=ALU.add),
                      reads=zreads + [("WC", lp), ("T", tc)], writes=[("T", tb)])

                def tail():
                    col_chunk_mm(l, 2 + 3 * j + 2, n, b0 + 2)
                    S.add("dve", lambda e, tb=tb, b0=b0, n=n: e.tensor_tensor(out=BIG[:, j, n * 512:(n + 1) * 512], in0=P[b0 + 2][:], in1=T[tb][:, 0:512], op=ALU.mult),
                          reads=[("P", b0 + 2), ("T", tb)], writes=[("BIG", j, 4 * n + t) for t in range(4)])
                return tail

        def pool_unit(l, c, n):
            lp = l % 2
            if True:
                bp = n % 2
                col_chunk_mm(l, c, n, 3)
                zc = 16 + n * 512
                S.add("act", lambda e, zc=zc: e.activation(out=ZP[:, zc:zc + 512], in_=P[3][:], func=AF.Copy),
                      reads=[("P", 3)], writes=[("Z", n)])
                zreads = [("Z", n)] + ([("Z", n - 1)] if n > 0 else ["ZPhalo"])
                lo = zc - 16
                ta, tb = next_T(), next_T()
                S.add("dve", lambda e, ta=ta, lo=lo: e.tensor_tensor(out=T[ta][:, 1:528], in0=ZP[:, lo + 1:lo + 528], in1=ZP[:, lo:lo + 527], op=ALU.add),
                      reads=zreads, writes=[("T", ta)])
                if c == 0:
                    cur, sh, st = ta, 2, 3
                    fin = tb
                else:
                    S.add("dve", lambda e, ta=ta, tb=tb: e.tensor_tensor(out=T[tb][:, 3:528], in0=T[ta][:, 3:528], in1=T[ta][:, 1:526], op=ALU.add),
                          reads=[("T", ta)], writes=[("T", tb)])
                    tcx = next_T()
                    S.add("dve", lambda e, tcx=tcx, tb=tb: e.tensor_tensor(out=T[tcx][:, 7:528], in0=T[tb][:, 7:528], in1=T[tb][:, 3:524], op=ALU.add),
                          reads=[("T", tb)], writes=[("T", tcx)])
                    cur, sh, st = tcx, 8, 15
                    fin = ta
                S.add("dve", lambda e, cur=cur, fin=fin, sh=sh, st=st: e.scalar_tensor_tensor(out=T[fin][:, st:528], in0=T[cur][:, st - sh:528 - sh], scalar=FLAG[:, 0:1],
                                                                                             in1=T[cur][:, st:528], op0=ALU.mult, op1=ALU.add),
                      reads=[("T", cur), "FLAG"], writes=[("T", fin)])
                S.add("dve", lambda e, fin=fin, bp=bp, zc=zc: e.scalar_tensor_tensor(out=DB[bp][:], in0=T[fin][:, 16:528], scalar=PINVW[:, c:c + 1],
                                                                                    in1=ZP[:, zc:zc + 512], op0=ALU.mult, op1=ALU.subtract),
                      reads=[("T", fin), "PINVW", ("Z", n)], writes=[("DB", bp)])
                if n == 0:
                    S.add("dve", lambda e, fin=fin, cur=cur: e.tensor_tensor(out=T[cur][:, 0:16], in0=T[fin][:, 16:32], in1=PINV[:, c, :], op=ALU.mult),
                          reads=[("T", fin), "PINV"], writes=[("T", cur)])
                    S.add("dve", lambda e, cur=cur, bp=bp, zc=zc: e.tensor_tensor(out=DB[bp][:, 0:16], in0=T[cur][:, 0:16], in1=ZP[:, zc:zc + 16], op=ALU.subtract),
                          reads=[("T", cur), ("Z", n), ("DB", bp)], writes=[("DB", bp)])

                def tail():
                    S.add("pe", lambda e, bp=bp: e.matmul(P[4][:], lhsT=WPBD[lp][:, c, :], rhs=DB[bp][:], start=True, stop=True),
                          reads=[("WPBD", lp), ("DB", bp)], writes=[("P", 4)])
                    S.add("act", lambda e, n=n: e.activation(out=BIG[:, 3 + c, n * 512:(n + 1) * 512], in_=P[4][:], func=AF.Identity, scale=PSC[lp][:, c:c + 1]),
                          reads=[("P", 4), ("PSC", lp)], writes=[("BIG", 3 + c, 4 * n + t) for t in range(4)])
                return tail

        def wo_tile(l, i):
            pl = plans[l]
            p = i % 2
            for nh in range(2):
                bank = 2 * p + nh
                for k in range(8):
                    slot, rk = wslot(pl["wo"][(k // 4) * 2 + nh])
                    wv = slot[:].rearrange("p (kc n) -> p kc n", n=512)
                    S.add("pe", lambda e, k=k, wv=wv, bank=bank: e.matmul(P[bank][:], lhsT=BIG[:, k, i * 128:(i + 1) * 128], rhs=wv[:, k % 4, :],
                                                                        start=(k == 0), stop=(k == 7)),
                          reads=[("BIG", k, i), rk], writes=[("P", bank)])
                S.add("dve", lambda e, nh=nh, bank=bank: e.scalar_tensor_tensor(out=X[:, i, nh * 512:(nh + 1) * 512], in0=X[:, i, nh * 512:(nh + 1) * 512], scalar=ALPHA,
                                                                              in1=P[bank][:], op0=ALU.mult, op1=ALU.add),
                      reads=[("X", i), ("P", bank)], writes=[("X", i)])

        def gate_up(l, m, ml, n, ctr):
            pl = plans[l]
            slot, rk = wslot(pl["gu"][m])
            wv = slot[:].rearrange("p (h kc n) -> p h kc n", h=2, n=128)
            b0 = 2 * (ctr % 2)
            for h in range(2):
                for k in range(8):
                    S.add("pe", lambda e, h=h, k=k: e.matmul(P[b0 + h][:], lhsT=wv[:, h, k, :], rhs=XT[:, k, n * 512:(n + 1) * 512], start=(k == 0), stop=(k == 7)),
                          reads=[("XT", 4 * n + t) for t in range(4)] + [rk], writes=[("P", b0 + h)])
            ta = next_T()
            S.add("act", lambda e: e.activation(out=T[ta][:, 0:512], in_=P[b0][:], func=AF.Silu), reads=[("P", b0)], writes=[("T", ta)])
            S.add("dve", lambda e: e.tensor_tensor(out=BIG[:, ml, n * 512:(n + 1) * 512], in0=P[b0 + 1][:], in1=T[ta][:, 0:512], op=ALU.mult),
                  reads=[("P", b0 + 1), ("T", ta)], writes=[("BIG", ml, 4 * n + t) for t in range(4)])

        def down_tile(l, si, i, first):
            pl = plans[l]
            m0, nm = SLABS[si]
            p = i % 2
            for nh in range(2):
                bank = 2 * p + nh
                for kk in range(nm):
                    slot, rk = wslot(pl["dn"][si][kk // 2])
                    wv = slot[:].rearrange("p (kc n) -> p kc n", n=1024)
                    S.add("pe", lambda e, kk=kk, wv=wv, bank=bank, nh=nh: e.matmul(P[bank][:], lhsT=BIG[:, kk, i * 128:(i + 1) * 128],
                                                                                 rhs=wv[:, kk % 2, nh * 512:(nh + 1) * 512], start=(kk == 0), stop=(kk == nm - 1)),
                          reads=[("BIG", kk, i), rk], writes=[("P", bank)])
                if first:
                    S.add("dve", lambda e, nh=nh, bank=bank: e.scalar_tensor_tensor(out=X[:, i, nh * 512:(nh + 1) * 512], in0=X[:, i, nh * 512:(nh + 1) * 512], scalar=ALPHA,
                                                                                  in1=P[bank][:], op0=ALU.mult, op1=ALU.add),
                          reads=[("X", i), ("P", bank)], writes=[("X", i)])
                else:
                    S.add("dve", lambda e, nh=nh, bank=bank: e.tensor_tensor(out=X[:, i, nh * 512:(nh + 1) * 512], in0=X[:, i, nh * 512:(nh + 1) * 512],
                                                                           in1=P[bank][:], op=ALU.add),
                          reads=[("X", i), ("P", bank)], writes=[("X", i)])

        outs = []
        load_small(layers[0])
        for i in range(NT):
            cast_transpose(i)
        for li, l in enumerate(layers):
            pl = plans[l]
            load_gb(l, 1)
            cv = pl["cv"]
            units = []
            for c in range(2):
                for n in range(NG):
                    rel = [cv[0]] if (c == 1 and n == NG - 1) else None
                    units.append((lambda c=c, n=n: pool_unit(l, c, n), rel))
            for j in range(3):
                for n in range(NG):
                    rel = {0: [cv[1]], 1: [cv[2], cv[3]], 2: [cv[4], cv[5]]}[j] if n == NG - 1 else None
                    units.append((lambda j=j, n=n: conv_unit(l, j, n), rel))
            ui = 0
            pending = [None]
            nit = NT + 3
            for t in range(nit):
                def sgu_part(t=t):
                    if t < NT:
                        sgu_a(l, t)
                        if t == NT - 1:
                            ring_release(pl["uv"])
                    if 0 <= t - 2 < NT:
                        sgu_c1(l, t - 2)
                    if 0 <= t - 1 < NT:
                        sgu_b(l, t - 1)

                def sgu_part2(t=t):
                    if 0 <= t - 2 < NT:
                        sgu_mix(l, t - 2)
                        sgu_c2(l, t - 2)
                    if 0 <= t - 3 < NT:
                        sgu_tr(l, t - 3)

                def unit_part(t=t):
                    nonlocal ui
                    target = min(len(units), -(-(t + 1) * len(units) // nit))
                    while ui < target:
                        fn, rel = units[ui]
                        if pending[0] is not None:
                            ptail, prel = pending[0]
                            ptail()
                            if prel:
                                ring_release(prel)
                        pending[0] = (fn(), rel)
                        ui += 1
                    if t == nit - 1 and pending[0] is not None:
                        ptail, prel = pending[0]
                        ptail()
                        if prel:
                            ring_release(prel)
                        pending[0] = None

                sgu_part()
                unit_part()
                sgu_part2()
            if dump is not None and dump[0] == "A":
                break
            gu_done = set()
            gu_left = {m: NG for m in range(NFF)}
            cnt["gu"] = 0

            def do_gate_up(m, ml, n):
                gate_up(l, m, ml, n, cnt["gu"])
                cnt["gu"] += 1
                gu_done.add((m, n))
                gu_left[m] -= 1
                if gu_left[m] == 0:
                    ring_release([pl["gu"][m]])

            for t in range(NT + 5):
                if 0 <= t - 2 < NT:
                    ln_norm(t - 2)
                if 0 <= t - 3 < NT:
                    ln_bias(t - 3)
                if 0 <= t - 4 < NT:
                    cast_stage(t - 4)
                if 0 <= t - 5 < NT:
                    transpose_stage(t - 5)
                if t < NT:
                    wo_tile(l, t)
                    ln_stats(t)
                if NT <= t < NT + 4:
                    ring_release([pl["wo"][t - NT]])
                if t >= NT and not (dump is not None and dump[0] == "B"):
                    emitted = 0
                    for m in range(SLABS[0][1]):
                        for n in range(NG):
                            if emitted >= 3 or (m, n) in gu_done:
                                continue
                            if (t - 5) < 4 * n + 3 or pl["gu"][m] not in ring_state["slot_of"]:
                                continue
                            do_gate_up(m, m, n)
                            emitted += 1
            if dump is not None and dump[0] == "B":
                break
            load_gb(l, 2)
            if li + 1 < len(layers):
                load_small(layers[li + 1])
            for si, (m0, nm) in enumerate(SLABS):
                for ml in range(nm):
                    m = m0 + ml
                    for n in range(NG):
                        if (m, n) not in gu_done:
                            do_gate_up(m, ml, n)
                last = si == len(SLABS) - 1
                if not last:
                    for i in range(NT):
                        down_tile(l, si, i, si == 0)
                else:
                    more = li + 1 < len(layers)
                    for t in range(NT + 5):
                        if 0 <= t - 2 < NT:
                            ln_norm(t - 2)
                        if 0 <= t - 3 < NT:
                            ln_bias(t - 3)
                            if not more:
                                i = t - 3
                                outs.append(S.add("sp", lambda e, i=i: e.dma_start(out=out_d[i * 128:(i + 1) * 128, :], in_=X[:, i, :]),
                                                  reads=[("X", i)], dma=True))
                        if more and 0 <= t - 4 < NT:
                            cast_stage(t - 4)
                        if more and 0 <= t - 5 < NT:
                            transpose_stage(t - 5)
                        if t < NT:
                            down_tile(l, si, t, si == 0)
                            ln_stats(t)
                ring_release(pl["dn"][si])
        if dump is not None:
            kind = dump[3]
            if kind == "BIG":
                outs.append(S.add("sp", lambda e: e.dma_start(out=dump_d, in_=BIG[:]),
                                  reads=[("BIG", c, i) for c in range(8) for i in range(NT)], dma=True))
            elif kind == "X":
                for i in range(NT):
                    outs.append(S.add("sp", lambda e, i=i: e.dma_start(out=dump_d[:, i, :], in_=X[:, i, :]), reads=[("X", i)], dma=True))
            elif kind == "XT":
                outs.append(S.add("sp", lambda e: e.dma_start(out=dump_d, in_=XT[:]), reads=[("XT", i) for i in range(NT)], dma=True))
        S.emit(nc, sems, dsems, outs)
    return nc


def _host_inputs(inputs, b):
    f = lambda a: np.ascontiguousarray(np.asarray(a, dtype=np.float32))
    m = {
        "x": f(inputs["x"][b]),
        "w_in": f(inputs["w_in"]),
        "w_convT": f(np.asarray(inputs["w_conv"]).reshape(DEPTH, 3, 3, 128).transpose(0, 3, 2, 1)),
        "w_pool": f(inputs["w_pool"]),
        "pool_scaleT": f(np.asarray(inputs["pool_scale"]).reshape(DEPTH, 2, 128).transpose(0, 2, 1)),
        "sgu_ln_g": f(inputs["sgu_ln_g"]),
        "w_spatialT": f(np.asarray(inputs["w_spatial"]).transpose(0, 3, 1, 2)),
        "b_spatialT": f(np.asarray(inputs["b_spatial"]).transpose(0, 2, 1)),
        "w_o": f(inputs["w_o"]),
        "ln1_g": f(inputs["ln1_g"]), "ln1_b": f(inputs["ln1_b"]),
        "ln2_g": f(inputs["ln2_g"]), "ln2_b": f(inputs["ln2_b"]),
        "w_gate_up": f(inputs["w_gate_up"]),
        "w_down": f(inputs["w_down"]),
    }
    m.update(_const_inputs())
    return m


_NC_CACHE = {}


def kernel(**inputs):
    key = "full"
    if key not in _NC_CACHE:
        _NC_CACHE[key] = build_program(list(range(DEPTH)))
    nc = _NC_CACHE[key]
    in_maps = [_host_inputs(inputs, b) for b in range(8)]
    res = run_bass_kernel_spmd(nc, in_maps, core_ids=list(range(8)))
    out = np.stack([np.asarray(r["out"], dtype=np.float32) for r in res.results], axis=0)
    return out
```
